# Optimizing a Trainium2 kernel written in Bass

```python
import math
import jax, jax.numpy as jnp
from jax import lax
import numpy as np

D_MODEL = 2048
BATCH = 1
SEQ = 8192
DEPTH = 4

N_MEM = 256
C_A = 1024
N_GROUPS_A = 8
CONV_A_WIDTH = 31
H_DIFF = 8
DH_DIFF = 64
QK_WIDTH = 2 * H_DIFF * DH_DIFF
V_WIDTH = H_DIFF * 2 * DH_DIFF
MIX_WIDTH = C_A + V_WIDTH
D_IN = 2 * C_A + 2 * QK_WIDTH + V_WIDTH
N_BUCKETS = 32
MAX_DISTANCE = 128
Q_BLOCK = 128
H_CROSS = 4
DH_CROSS = 128
D_FF = 5632
CONV_F_WIDTH = 3
NORM_EPS = 1e-6
SUBLN_EPS = 1e-5
NEG_INF = -1e30

kernel_name = "hybrid_conformer_diffattn_trunk"


def rms_norm(x, g, eps=NORM_EPS):
    xf = x.astype(jnp.float32)
    y = xf * lax.rsqrt(jnp.mean(xf * xf, axis=-1, keepdims=True) + eps)
    return (y * g.astype(jnp.float32)).astype(x.dtype)


def layer_norm(x, g, b, eps=NORM_EPS):
    xf = x.astype(jnp.float32)
    mu = jnp.mean(xf, axis=-1, keepdims=True)
    xc = xf - mu
    y = xc * lax.rsqrt(jnp.mean(xc * xc, axis=-1, keepdims=True) + eps)
    return (y * g.astype(jnp.float32) + b.astype(jnp.float32)).astype(x.dtype)


def causal_depthwise_conv(u, w, b):
    k_w, c = w.shape
    y = lax.conv_general_dilated(
        u, w[:, None, :].astype(u.dtype), window_strides=(1,), padding=[(k_w - 1, 0)],
        dimension_numbers=("NWC", "WIO", "NWC"), feature_group_count=c)
    return y + b.astype(u.dtype)


def t5_causal_bucket(n):
    max_exact = N_BUCKETS // 2
    nf = jnp.maximum(n, 1).astype(jnp.float32)
    large = max_exact + (jnp.log(nf / max_exact) / math.log(MAX_DISTANCE / max_exact)
                         * (N_BUCKETS - max_exact)).astype(jnp.int32)
    large = jnp.minimum(large, N_BUCKETS - 1)
    return jnp.where(n < max_exact, n, large)


def diff_attention(q, k, v, lam, bias_dist_t):
    b, s = q.shape[0], q.shape[1]
    n_blocks = s // Q_BLOCK
    scale = DH_DIFF ** -0.5
    q_blocks = q.reshape(b, n_blocks, Q_BLOCK, 2 * H_DIFF, DH_DIFF).transpose(1, 0, 2, 3, 4)
    k_pos = jnp.arange(s)

    def one_block(args):
        q_blk, i = args
        q_pos = i * Q_BLOCK + jnp.arange(Q_BLOCK)
        rel = q_pos[:, None] - k_pos[None, :]
        causal = rel >= 0
        bias = bias_dist_t[:, jnp.clip(rel, 0, s - 1)].astype(jnp.float32)
        logits = jnp.einsum("bqnd,bknd->bnqk", q_blk, k).astype(jnp.float32) * scale
        logits = logits.reshape(b, H_DIFF, 2, Q_BLOCK, s) + bias[None, :, None]
        logits = jnp.where(causal[None, None, None], logits, NEG_INF)
        p = jax.nn.softmax(logits, axis=-1)
        w = p[:, :, 0] - lam * p[:, :, 1]
        return jnp.einsum("bhqk,bkhe->bqhe", w.astype(v.dtype), v)

    out = lax.map(one_block, (q_blocks, jnp.arange(n_blocks)))
    return out.transpose(1, 0, 2, 3, 4).reshape(b, s, H_DIFF, 2 * DH_DIFF)


def setup_inputs(seed: int = 0) -> dict:
    key = jax.random.key(seed)
    ks = iter(jax.random.split(key, 32))
    f32 = jnp.float32

    def nrm(shape, scale):
        return jax.random.normal(next(ks), shape, f32) * scale

    def gain(shape):
        return 1.0 + 0.01 * jax.random.normal(next(ks), shape, f32)

    return {
        "x": nrm((BATCH, SEQ, D_MODEL), 1.0),
        "mem": nrm((BATCH, N_MEM, D_MODEL), 1.0),
        "rel_bias_table": nrm((N_BUCKETS, H_DIFF), 0.5),
        "g_mix": gain((DEPTH, D_MODEL)),
        "w_in": nrm((DEPTH, D_MODEL, D_IN), D_MODEL ** -0.5),
        "conv_a_w": nrm((DEPTH, CONV_A_WIDTH, C_A), CONV_A_WIDTH ** -0.5),
        "conv_a_b": nrm((DEPTH, C_A), 0.01),
        "ln_a_g": gain((DEPTH, C_A)),
        "ln_a_b": nrm((DEPTH, C_A), 0.01),
        "diff_lambda": nrm((DEPTH, 4, DH_DIFF), 0.1),
        "subln_g": gain((DEPTH, 2 * DH_DIFF)),
        "w_out": nrm((DEPTH, MIX_WIDTH, D_MODEL), MIX_WIDTH ** -0.5),
        "g_cross": gain((DEPTH, D_MODEL)),
        "g_mem": gain((DEPTH, D_MODEL)),
        "w_cq": nrm((DEPTH, D_MODEL, H_CROSS * DH_CROSS), D_MODEL ** -0.5),
        "w_ckv": nrm((DEPTH, D_MODEL, 2 * H_CROSS * DH_CROSS), D_MODEL ** -0.5),
        "w_co": nrm((DEPTH, H_CROSS * DH_CROSS, D_MODEL), (H_CROSS * DH_CROSS) ** -0.5),
        "g_ffn": gain((DEPTH, D_MODEL)),
        "w_up": nrm((DEPTH, D_MODEL, 2 * D_FF), D_MODEL ** -0.5),
        "conv_f_w": nrm((DEPTH, CONV_F_WIDTH, 2 * D_FF), CONV_F_WIDTH ** -0.5),
        "conv_f_b": nrm((DEPTH, 2 * D_FF), 0.01),
        "w_down": nrm((DEPTH, D_FF, D_MODEL), D_FF ** -0.5),
        "g_final": gain((D_MODEL,)),
    }


def reference(x, mem, rel_bias_table, g_mix, w_in, conv_a_w, conv_a_b, ln_a_g, ln_a_b,
              diff_lambda, subln_g, w_out, g_cross, g_mem, w_cq, w_ckv, w_co,
              g_ffn, w_up, conv_f_w, conv_f_b, w_down, g_final):
    b, s, _ = x.shape
    bias_dist_t = rel_bias_table[t5_causal_bucket(jnp.arange(s, dtype=jnp.int32))].T

    for l in range(DEPTH):
        h = rms_norm(x, g_mix[l])
        proj = h @ w_in[l]
        a_in = proj[..., :2 * C_A]
        o = 2 * C_A
        q = proj[..., o:o + QK_WIDTH].reshape(b, s, 2 * H_DIFF, DH_DIFF)
        k = proj[..., o + QK_WIDTH:o + 2 * QK_WIDTH].reshape(b, s, 2 * H_DIFF, DH_DIFF)
        v = proj[..., o + 2 * QK_WIDTH:].reshape(b, s, H_DIFF, 2 * DH_DIFF)

        a = a_in[..., :C_A] * jax.nn.sigmoid(a_in[..., C_A:])
        a = causal_depthwise_conv(a, conv_a_w[l], conv_a_b[l])
        a = jax.nn.silu(layer_norm(a, ln_a_g[l], ln_a_b[l]))

        lam_p = diff_lambda[l].astype(jnp.float32)
        lambda_init = 0.8 - 0.6 * math.exp(-0.3 * l)
        lam = (jnp.exp(jnp.sum(lam_p[0] * lam_p[1])) - jnp.exp(jnp.sum(lam_p[2] * lam_p[3]))
               + lambda_init)
        d = diff_attention(q, k, v, lam, bias_dist_t)
        d = rms_norm(d, subln_g[l], SUBLN_EPS) * (1.0 - lambda_init)
        d = d.reshape(b, s, V_WIDTH)

        x = x + jnp.concatenate([a, d], axis=-1) @ w_out[l]

        hc = rms_norm(x, g_cross[l])
        m = rms_norm(mem, g_mem[l])
        cq = (hc @ w_cq[l]).reshape(b, s, H_CROSS, DH_CROSS)
        ckv = (m @ w_ckv[l]).reshape(b, N_MEM, 2, H_CROSS, DH_CROSS)
        ck, cv = ckv[:, :, 0], ckv[:, :, 1]
        cl = jnp.einsum("bqhd,bmhd->bhqm", cq, ck).astype(jnp.float32) * DH_CROSS ** -0.5
        cp = jax.nn.softmax(cl, axis=-1).astype(cv.dtype)
        co = jnp.einsum("bhqm,bmhd->bqhd", cp, cv).reshape(b, s, H_CROSS * DH_CROSS)
        x = x + co @ w_co[l]

        hf = rms_norm(x, g_ffn[l])
        u = causal_depthwise_conv(hf @ w_up[l], conv_f_w[l], conv_f_b[l])
        x = x + (jax.nn.silu(u[..., :D_FF]) * u[..., D_FF:]) @ w_down[l]

    return rms_norm(x, g_final)
```

```python
import contextlib
import math
import numpy as np
import ml_dtypes
import concourse.bass as bass
import concourse.mybir as mybir
from concourse.bass_utils import run_bass_kernel_spmd

F32 = mybir.dt.float32
BF16 = mybir.dt.bfloat16
AF = mybir.ActivationFunctionType
ALU = mybir.AluOpType

NCORES = 8
S = 8192
D = 2048
KC = 16
TOK = 1024
DFF = 5632
NEG = -1e30


class T:
    __slots__ = ("name", "h", "lw", "rd", "sem", "cnt")

    def __init__(self, name, h=None):
        self.name = name
        self.h = h
        self.lw = None
        self.rd = []
        self.sem = None
        self.cnt = 0


class K:
    def __init__(self, nc, stack):
        self.nc = nc
        self.stack = stack
        self.semstack = stack
        self.eng = {}
        for name, e in (("pe", nc.tensor), ("act", nc.scalar), ("dve", nc.vector),
                        ("pool", nc.gpsimd), ("sp", nc.sync)):
            sem = stack.enter_context(nc.semaphore("s_" + name))
            self.eng[name] = {"e": e, "sem": sem, "cnt": 0, "waited": {}}
        self.tiles = []
        self.sempool = {}

    def tilesem(self, t):
        if t.sem is None:
            if t.name not in self.sempool:
                self.sempool[t.name] = [self.semstack.enter_context(self.nc.semaphore("d_" + t.name)), 0]
            t.sem, t.cnt = self.sempool[t.name]
        return t.sem

    def tilesem_inc(self, t):
        t.cnt += 16
        self.sempool[t.name][1] = t.cnt

    def _uid(self, name):
        self.uid = getattr(self, "uid", 0) + 1
        return "%s_u%d" % (name, self.uid)

    def sbuf(self, name, shape, dt):
        t = T(name, self.stack.enter_context(self.nc.sbuf_tensor(self._uid(name), shape, dt)))
        self.tiles.append(t)
        return t

    def psum(self, name, shape, dt=F32):
        t = T(name, self.stack.enter_context(self.nc.psum_tensor(self._uid(name), shape, dt)))
        self.tiles.append(t)
        return t

    def dram(self, name, shape, dt, kind):
        t = T(name, self.nc.dram_tensor(name, shape, dt, kind=kind).ap())
        self.tiles.append(t)
        return t

    def _deps(self, reads, writes):
        deps = {}

        def add(sv):
            if sv is None:
                return
            s, v = sv
            if id(s) not in deps or deps[id(s)][1] < v:
                deps[id(s)] = (s, v)
        for t in reads:
            add(t.lw)
        for t in writes:
            add(t.lw)
            for r in t.rd:
                add(r)
        return list(deps.values())

    def _wait(self, en, deps):
        E = self.eng[en]
        for s, v in deps:
            if E["waited"].get(id(s), 0) < v:
                E["e"].wait_ge(s, v)
                E["waited"][id(s)] = v

    def _mark(self, sv, reads, writes):
        for t in writes:
            t.lw = sv
            t.rd = []
        for t in reads:
            if t not in writes:
                t.rd.append(sv)
                if len(t.rd) > 16:
                    best = {}
                    for s, v in t.rd:
                        if id(s) not in best or best[id(s)][1] < v:
                            best[id(s)] = (s, v)
                    t.rd = list(best.values())

    def op(self, en, fn, reads=(), writes=()):
        reads = list(reads)
        writes = list(writes)
        self._wait(en, self._deps(reads, writes))
        E = self.eng[en]
        ins = fn(E["e"])
        ins.then_inc(E["sem"], 1)
        E["cnt"] += 1
        self._mark((E["sem"], E["cnt"]), reads, writes)

    def mm(self, out_t, out_ap, pairs, reads=(), start=True, stop=True):
        reads = list(reads)
        self._wait("pe", self._deps(reads, [out_t] if start else []))
        E = self.eng["pe"]
        n = len(pairs)
        ins = None
        for i, (l, r) in enumerate(pairs):
            ins = E["e"].matmul(out_ap, l, r, start=(start and i == 0), stop=(stop and i == n - 1))
        ins.then_inc(E["sem"], 1)
        E["cnt"] += 1
        self._mark((E["sem"], E["cnt"]), reads, [out_t])

    def dma(self, q, out_ap, in_ap, reads, writes, semtile):
        reads = list(reads)
        writes = list(writes)
        self._wait(q, self._deps(reads, writes))
        st = semtile
        self.tilesem(st)
        ins = self.eng[q]["e"].dma_start(out=out_ap, in_=in_ap)
        ins.then_inc(st.sem, 16)
        self.tilesem_inc(st)
        self._mark((st.sem, st.cnt), reads, writes)

    @contextlib.contextmanager
    def scope(self, local_only=False):
        outer = self.stack
        n0 = len(self.tiles)
        with contextlib.ExitStack() as inner:
            self.stack = inner
            yield
            for en in self.eng:
                self.finish(en, self.tiles[n0:] if local_only else None)
            self.stack = outer
        del self.tiles[n0:]

    def finish(self, q="sp", tiles=None):
        deps = {}
        for t in (self.tiles if tiles is None else tiles):
            for sv in ([t.lw] if t.lw else []) + t.rd:
                s, v = sv
                if id(s) not in deps or deps[id(s)][1] < v:
                    deps[id(s)] = (s, v)
        self._wait(q, list(deps.values()))


class Ctx:
    pass


def setup_common(k, c, nslots=3):
    c.ones32 = k.sbuf("ones32", [128, 128], F32)
    k.op("dve", lambda e: e.memset(c.ones32.h[:], 1.0), writes=[c.ones32])
    c.onesb = k.sbuf("onesb", [128, 128], BF16)
    k.op("dve", lambda e: e.memset(c.onesb.h[:], 1.0), writes=[c.onesb])
    c.epsn = k.sbuf("epsn", [128, 1], F32)
    k.op("dve", lambda e: e.memset(c.epsn.h[:], 1e-6), writes=[c.epsn])
    c.eps5 = k.sbuf("eps5", [128, 1], F32)
    k.op("dve", lambda e: e.memset(c.eps5.h[:], 1e-5), writes=[c.eps5])
    c.ps = [k.psum("ps%d" % i, [128, 512], F32) for i in range(4)]
    c.psi = 0
    c.slots = []
    c.sloti = 0
    c.sq = [k.sbuf("sq%d" % i, [128, 512], F32) for i in range(2)]
    c.rb = k.sbuf("rb", [128, 512], F32)


def next_ps(c):
    p = c.ps[c.psi % len(c.ps)]
    c.psi += 1
    return p


def load_slot(k, c, W, k0, nk, col0, ncols=512):
    s = c.slots[c.sloti % len(c.slots)]
    c.sloti += 1
    src = W.h[k0 * 128:(k0 + nk) * 128, col0:col0 + ncols].rearrange("(k p) n -> p k n", p=128)
    dst = s.h[:, 0:nk * 512].rearrange("p (k n) -> p k n", n=512)[:, :, 0:ncols]
    k.dma("pool", dst, src, reads=[W], writes=[s], semtile=s)
    return s


def proj_fm(k, c, actT, kchunks, W, ocs, ranges, handler):
    ocs = list(ocs)
    groups = [ocs[i:i + 4] for i in range(0, len(ocs), 4)]
    kslices = [(k0, min(16, kchunks - k0)) for k0 in range(0, kchunks, 16)]
    jobs = [(g, ks) for g in groups for ks in kslices]
    multi = len(kslices) > 1
    if multi:
        assert 4 * len(ranges) <= len(c.ps)
    nxt = load_slot(k, c, W, jobs[0][1][0], jobs[0][1][1], jobs[0][0][0] * 128, 128 * len(jobs[0][0]))
    for i, (g, (k0, nk)) in enumerate(jobs):
        s = nxt
        if i + 1 < len(jobs):
            g2, (k02, nk2) = jobs[i + 1]
            nxt = load_slot(k, c, W, k02, nk2, g2[0] * 128, 128 * len(g2))
        for j, oc in enumerate(g):
            for ri, (t0, n) in enumerate(ranges):
                pairs = [(s.h[:, kc * 512 + j * 128:kc * 512 + (j + 1) * 128], actT.h[:, k0 + kc, t0:t0 + n]) for kc in range(nk)]
                if not multi:
                    p = next_ps(c)
                    k.mm(p, p.h[:, 0:n], pairs, reads=[s, actT])
                    handler(oc, ri, t0, n, p)
                else:
                    p = c.ps[j * len(ranges) + ri]
                    last = (k0 + nk == kchunks)
                    k.mm(p, p.h[:, 0:n], pairs, reads=[s, actT], start=(k0 == 0), stop=last)
                    if last:
                        handler(oc, ri, t0, n, p)


def rmsnorm_fm(k, c, xT, nch, gcol, outT, ranges, inv_n, eps_t, out_writes=None):
    for (t0, n) in ranges:
        p = next_ps(c)
        k._wait("pe", k._deps([], [p]))
        for ch in range(nch):
            sq = c.sq[ch % 2]
            k.op("act", lambda e, ch=ch, sq=sq: e.activation(out=sq.h[:, 0:n], in_=xT.h[:, ch, t0:t0 + n], func=AF.Square),
                 reads=[xT], writes=[sq])
            k.mm(p, p.h[:, 0:n], [(c.ones32.h[:], sq.h[:, 0:n])], reads=[c.ones32, sq],
                 start=(ch == 0), stop=(ch == nch - 1))
        k.op("act", lambda e: e.activation(out=c.rb.h[:, 0:n], in_=p.h[:, 0:n], func=AF.Sqrt, bias=eps_t.h[:, 0:1], scale=inv_n),
             reads=[p, eps_t], writes=[c.rb])
        k.op("dve", lambda e: e.reciprocal(out=c.rb.h[:, 0:n], in_=c.rb.h[:, 0:n]), reads=[c.rb], writes=[c.rb])
        for ch in range(nch):
            k.op("dve", lambda e, ch=ch: e.scalar_tensor_tensor(out=outT.h[:, ch, t0:t0 + n], in0=xT.h[:, ch, t0:t0 + n],
                                                                  scalar=gcol.h[:, ch:ch + 1], in1=c.rb.h[:, 0:n],
                                                                  op0=ALU.mult, op1=ALU.mult),
                 reads=[xT, gcol, c.rb], writes=[outT])


def load(k, q, tile, ap_sb, dram, ap_dr):
    k.dma(q, ap_sb, ap_dr, reads=[dram], writes=[tile], semtile=tile)


U32 = mybir.dt.uint32


class _HV:
    def __init__(self, f):
        self.f = f

    def __getitem__(self, idx):
        return self.f(idx)


class _View(T):
    def __init__(self, base, fn):
        object.__setattr__(self, "base", base)
        object.__setattr__(self, "fn", fn)

    def __getattr__(self, name):
        return getattr(object.__getattribute__(self, "base"), name)

    def __setattr__(self, name, val):
        setattr(object.__getattribute__(self, "base"), name, val)

    @property
    def h(self):
        b = object.__getattribute__(self, "base")
        fn = object.__getattribute__(self, "fn")
        return _HV(lambda idx: fn(b.h, idx))


def ColView(base, i):
    return _View(base, lambda h, idx: h[idx[0], i, idx[1]])


def ShiftView(base, off):
    return _View(base, lambda h, idx: h[idx[0], idx[1], idx[2].start + off:idx[2].stop + off])


def slots_alloc(k, c):
    c.slots = [k.sbuf("wslot%d" % i, [128, 16 * 512], BF16) for i in range(2)]
    c.sloti = 0


def collective(k, st, sems, name, in_t, out_t):
    if name not in sems:
        sems[name] = [st.enter_context(k.nc.semaphore("cc_" + name)), 0]
    k._wait("pool", k._deps([in_t], [out_t]))
    ins = k.nc.gpsimd.collective_compute("AllGather", ALU.bypass, replica_groups=[list(range(NCORES))],
                                         ins=[in_t.h], outs=[out_t.h])
    ins.then_inc(sems[name][0])
    sems[name][1] += 1
    sv = (sems[name][0], sems[name][1])
    k._mark(sv, [in_t], [out_t])


def gather_rows(k, out_tile, out_ap, src_t, idx_t, idx_ap):
    k._wait("pool", k._deps([src_t, idx_t], [out_tile]))
    ins = k.nc.gpsimd.indirect_dma_start(out=out_ap, out_offset=None, in_=src_t.h[:, :],
                                         in_offset=bass.IndirectOffsetOnAxis(ap=idx_ap, axis=0))
    k.tilesem(out_tile)
    ins.then_inc(out_tile.sem, 16)
    k.tilesem_inc(out_tile)
    k._mark((out_tile.sem, out_tile.cnt), [src_t, idx_t], [out_tile])


def build_fused(depth):
    NT = TOK + 2
    nc = bass.Bass("TRN2", target_bir_lowering=False)
    with contextlib.ExitStack() as st:
        k = K(nc, st)
        c = Ctx()
        EI = "ExternalInput"
        xT_d = k.dram("xT", [16, 128, TOK], F32, EI)
        memT_d = k.dram("memT", [16, 128, 256], F32, EI)
        win_d = k.dram("w_in", [depth, D, 5120], F32, EI)
        wout_d = k.dram("w_out", [depth, D, D], F32, EI)
        wcq_d = k.dram("w_cq", [depth, D, 512], F32, EI)
        wckv_d = k.dram("w_ckv", [depth, D, 1024], F32, EI)
        wco_d = k.dram("w_co", [depth, 512, D], F32, EI)
        wup_d = k.dram("w_up", [depth, D, 2 * DFF], F32, EI)
        wdn_d = k.dram("w_down", [depth, DFF, D], F32, EI)
        gc_d = k.dram("gcols", [depth, 128, 5, 16], F32, EI)
        cw_d = k.dram("convw", [depth, 128, 8, 31], F32, EI)
        cv_d = k.dram("cvec", [depth, 128, 3, 8], F32, EI)
        lm_d = k.dram("lamp", [depth, 128, 256], F32, EI)
        sg_d = k.dram("sublng", [depth, 128, 1], F32, EI)
        cf_d = k.dram("convf", [depth, 128, 88, 4], F32, EI)
        bm_d = k.dram("bm", [5, 128, 512], F32, EI)
        mk_d = k.dram("mask", [5, 128, 512], F32, EI)
        cfar_d = k.dram("cfar", [128, 1], F32, EI)
        flag_d = k.dram("flag", [128, 1], F32, EI)
        idxq_d = k.dram("idxq", [128, 24], U32, EI)
        idxd_d = k.dram("idxd", [128, 8], U32, EI)
        idxh_d = k.dram("idxh", [128, 1], U32, EI)
        yo_d = k.dram("yoT", [16, 128, TOK], F32, "ExternalOutput")

        def idram(name, shape, dt):
            t = T(name, nc.dram_tensor(name, shape, dt).ap())
            k.tiles.append(t)
            return t
        aglu_s = idram("aglu_s", [8, 128, TOK], F32)
        agi1 = [idram("agi1_%d" % l, [3072, TOK], BF16) for l in range(depth)]
        ago1 = [idram("ago1_%d" % l, [3072 * NCORES, TOK], BF16) for l in range(depth)]
        agi1b = [idram("agi1b_%d" % l, [128, 240], F32) for l in range(depth)]
        ago1b = [idram("ago1b_%d" % l, [128 * NCORES, 240], F32) for l in range(depth)]
        agi2 = [idram("agi2_%d" % l, [1024, TOK], BF16) for l in range(depth)]
        ago2 = [idram("ago2_%d" % l, [1024 * NCORES, TOK], BF16) for l in range(depth)]
        agi3 = [idram("agi3_%d" % l, [128, 32], BF16) for l in range(depth)]
        ago3 = [idram("ago3_%d" % l, [128 * NCORES, 32], BF16) for l in range(depth)]
        ccsems = {}

        setup_common(k, c)
        ident = k.sbuf("ident", [128, 128], BF16)
        k.op("pool", lambda e: e.memset(ident.h[:], 1.0), writes=[ident])
        k.op("pool", lambda e: e.affine_select(out=ident.h[:], in_=ident.h[:], pattern=[[-1, 128]], compare_op=ALU.is_equal,
                                               fill=0.0, base=0, channel_multiplier=1), reads=[ident], writes=[ident])
        xT = k.sbuf("xTs", [128, 16, TOK], F32)
        hT = k.sbuf("hT", [128, 16, NT], BF16)
        hTo = ShiftView(hT, 2)
        bm = k.sbuf("bms", [128, 5, 512], F32)
        cfar = k.sbuf("cfars", [128, 1], F32)
        flag = k.sbuf("flags", [128, 1], F32)
        idxq = k.sbuf("idxqs", [128, 24], U32)
        idxd = k.sbuf("idxds", [128, 8], U32)
        idxh = k.sbuf("idxhs", [128, 1], U32)
        gcs = k.sbuf("gcs", [128, 5, 16], F32)
        load(k, "sp", cfar, cfar.h[:], cfar_d, cfar_d.h[:, :])
        load(k, "sp", flag, flag.h[:], flag_d, flag_d.h[:, :])
        load(k, "sp", idxq, idxq.h[:], idxq_d, idxq_d.h[:, :])
        load(k, "sp", idxd, idxd.h[:], idxd_d, idxd_d.h[:, :])
        load(k, "sp", idxh, idxh.h[:], idxh_d, idxh_d.h[:, :])
        for ch in range(16):
            load(k, "sp", xT, xT.h[:, ch, :], xT_d, xT_d.h[ch, :, :])
        with k.scope():
            mk = k.sbuf("mks", [128, 5, 512], F32)
            for r in range(5):
                load(k, "sp", bm, bm.h[:, r, :], bm_d, bm_d.h[r, :, :])
                load(k, "sp", mk, mk.h[:, r, :], mk_d, mk_d.h[r, :, :])
            for r in range(5):
                k.op("dve", lambda e, r=r: e.tensor_tensor(out=bm.h[:, r, :], in0=bm.h[:, r, :], in1=mk.h[:, r, :], op=ALU.add),
                     reads=[bm, mk], writes=[bm])
        R2 = [(0, 512), (512, 512)]

        def add_x(oc, ri, t0, n, p):
            k.op("dve", lambda e: e.tensor_tensor(out=xT.h[:, oc, t0:t0 + n], in0=xT.h[:, oc, t0:t0 + n], in1=p.h[:, 0:n], op=ALU.add),
                 reads=[xT, p], writes=[xT])

        for l in range(depth):
            lam_init = 0.8 - 0.6 * math.exp(-0.3 * l)
            load(k, "sp", gcs, gcs.h[:], gc_d, gc_d.h[l, :, :, :])
            g_mix, g_cross, g_mem, g_ffn, g_fin = [ColView(gcs, i) for i in range(5)]
            with k.scope():
                slots_alloc(k, c)
                abuf = k.sbuf("abuf", [128, 8, TOK], F32)
                sig = [k.sbuf("sig%d" % i, [128, 512], F32) for i in range(2)]
                ob = [k.sbuf("ob%d" % i, [128, 512], BF16) for i in range(3)]
                rmsnorm_fm(k, c, xT, 16, g_mix, hTo, R2, 1.0 / D, c.epsn)
                cnt = {"s": 0, "o": 0}
                a1v = agi1[l].h.rearrange("(c p) t -> c p t", p=128)

                def handlerA(oc, ri, t0, n, p):
                    if oc < 8:
                        k.op("act", lambda e: e.copy(out=abuf.h[:, oc, t0:t0 + n], in_=p.h[:, 0:n]), reads=[p], writes=[abuf])
                    elif oc < 16:
                        sg = sig[cnt["s"] % 2]
                        cnt["s"] += 1
                        k.op("act", lambda e: e.activation(out=sg.h[:, 0:n], in_=p.h[:, 0:n], func=AF.Sigmoid), reads=[p], writes=[sg])
                        k.op("dve", lambda e: e.tensor_tensor(out=sg.h[:, 0:n], in0=sg.h[:, 0:n], in1=abuf.h[:, oc - 8, t0:t0 + n], op=ALU.mult),
                             reads=[sg, abuf], writes=[sg])
                        k.dma("sp", aglu_s.h[oc - 8, :, t0:t0 + n], sg.h[:, 0:n], reads=[sg], writes=[aglu_s], semtile=sg)
                        if ri == 1:
                            k.dma("sp", agi1b[l].h[:, (oc - 8) * 30:(oc - 8) * 30 + 30], sg.h[:, 482:512], reads=[sg], writes=[agi1b[l]], semtile=sg)
                    else:
                        o = ob[cnt["o"] % 3]
                        cnt["o"] += 1
                        k.op("act", lambda e: e.copy(out=o.h[:, 0:n], in_=p.h[:, 0:n]), reads=[p], writes=[o])
                        k.dma("sp", a1v[oc - 16, :, t0:t0 + n], o.h[:, 0:n], reads=[o], writes=[agi1[l]], semtile=o)
                proj_fm(k, c, hTo, 16, T("w", win_d.h[l]), range(40), R2, handlerA)
                collective(k, st, ccsems, "e1b", agi1b[l], ago1b[l])
            with k.scope():
                cw = k.sbuf("cw", [128, 8, 31], F32)
                cvec = k.sbuf("cvecs", [128, 3, 8], F32)
                load(k, "sp", cw, cw.h[:], cw_d, cw_d.h[l, :, :, :])
                load(k, "sp", cvec, cvec.h[:], cv_d, cv_d.h[l, :, :, :])
                halo = k.sbuf("halo", [128, 240], F32)
                gather_rows(k, halo, halo.h[:, :], ago1b[l], idxh, idxh.h[:, 0:1])
                acc = k.sbuf("acc", [128, 8, TOK], F32)
                agb = [k.sbuf("agb%d" % i, [128, TOK + 30], F32) for i in range(8)]
                for ch in range(8):
                    ab = agb[ch]
                    load(k, "sp", ab, ab.h[:, 30:30 + TOK], aglu_s, aglu_s.h[ch, :, :])
                    k.op("dve", lambda e, ch=ch, ab=ab: e.tensor_scalar(out=ab.h[:, 0:30], in0=halo.h[:, ch * 30:ch * 30 + 30], scalar1=flag.h[:, 0:1],
                                                                          scalar2=1.0, op0=ALU.mult, op1=ALU.mult), reads=[halo, flag, ab], writes=[ab])
                k._wait("pool", k._deps(agb + [cw, cvec, halo], []))
                collective(k, st, ccsems, "e1", agi1[l], ago1[l])
                for ch in range(8):
                    ab = agb[ch]
                    k.op("act", lambda e, ch=ch, ab=ab: e.activation(out=acc.h[:, ch, :], in_=ab.h[:, 30:30 + TOK], func=AF.Identity,
                                                                     bias=cvec.h[:, 0, ch:ch + 1], scale=cw.h[:, ch, 30:31]),
                         reads=[ab, cvec, cw], writes=[acc])
                    for j in range(30):
                        k.op("dve", lambda e, ch=ch, ab=ab, j=j: e.scalar_tensor_tensor(
                            out=acc.h[:, ch, :], in0=ab.h[:, j:j + TOK], scalar=cw.h[:, ch, j:j + 1], in1=acc.h[:, ch, :],
                            op0=ALU.mult, op1=ALU.add), reads=[ab, cw, acc], writes=[acc])
                mu = k.sbuf("mu", [128, 512], F32)
                tmp = k.sbuf("tmp", [128, 512], F32)
                for (t0, n) in R2:
                    p1 = next_ps(c)
                    p2 = next_ps(c)
                    for ch in range(8):
                        k.mm(p1, p1.h[:, 0:n], [(c.ones32.h[:], acc.h[:, ch, t0:t0 + n])], reads=[c.ones32, acc], start=(ch == 0), stop=(ch == 7))
                    for ch in range(8):
                        sq = c.sq[ch % 2]
                        k.op("act", lambda e, ch=ch, sq=sq: e.activation(out=sq.h[:, 0:n], in_=acc.h[:, ch, t0:t0 + n], func=AF.Square), reads=[acc], writes=[sq])
                        k.mm(p2, p2.h[:, 0:n], [(c.ones32.h[:], sq.h[:, 0:n])], reads=[c.ones32, sq], start=(ch == 0), stop=(ch == 7))
                    k.op("act", lambda e: e.activation(out=mu.h[:, 0:n], in_=p1.h[:, 0:n], func=AF.Identity, scale=1.0 / 1024), reads=[p1], writes=[mu])
                    k.op("dve", lambda e: e.tensor_tensor(out=tmp.h[:, 0:n], in0=mu.h[:, 0:n], in1=mu.h[:, 0:n], op=ALU.mult), reads=[mu], writes=[tmp])
                    k.op("dve", lambda e: e.scalar_tensor_tensor(out=tmp.h[:, 0:n], in0=p2.h[:, 0:n], scalar=1.0 / 1024, in1=tmp.h[:, 0:n],
                                                                 op0=ALU.mult, op1=ALU.subtract), reads=[p2, tmp], writes=[tmp])
                    k.op("act", lambda e: e.activation(out=c.rb.h[:, 0:n], in_=tmp.h[:, 0:n], func=AF.Sqrt, bias=c.epsn.h[:, 0:1], scale=1.0),
                         reads=[tmp, c.epsn], writes=[c.rb])
                    k.op("dve", lambda e: e.reciprocal(out=c.rb.h[:, 0:n], in_=c.rb.h[:, 0:n]), reads=[c.rb], writes=[c.rb])
                    for ch in range(8):
                        k.op("dve", lambda e, ch=ch: e.tensor_tensor(out=tmp.h[:, 0:n], in0=acc.h[:, ch, t0:t0 + n], in1=mu.h[:, 0:n], op=ALU.subtract),
                             reads=[acc, mu], writes=[tmp])
                        k.op("dve", lambda e: e.tensor_tensor(out=tmp.h[:, 0:n], in0=tmp.h[:, 0:n], in1=c.rb.h[:, 0:n], op=ALU.mult),
                             reads=[tmp, c.rb], writes=[tmp])
                        k.op("act", lambda e, ch=ch: e.activation(out=hT.h[:, ch, 2 + t0:2 + t0 + n], in_=tmp.h[:, 0:n], func=AF.Silu,
                                                                  bias=cvec.h[:, 2, ch:ch + 1], scale=cvec.h[:, 1, ch:ch + 1]),
                             reads=[tmp, cvec], writes=[hT])
            with k.scope():
                qT = k.sbuf("qTs", [128, S], BF16)
                kT = k.sbuf("kTs", [128, S], BF16)
                vv = k.sbuf("vs", [128, 64, 128], BF16)
                lamp = k.sbuf("lamps", [128, 256], F32)
                sgc = k.sbuf("sgc", [128, 1], F32)
                load(k, "sp", lamp, lamp.h[:], lm_d, lm_d.h[l, :, :])
                load(k, "sp", sgc, sgc.h[:], sg_d, sg_d.h[l, :, :])
                qB = k.sbuf("qBs", [128, S], BF16)
                for r in range(8):
                    gather_rows(k, qT, qT.h[:, r * TOK:(r + 1) * TOK], ago1[l], idxq, idxq.h[:, 3 * r:3 * r + 1])
                    gather_rows(k, qB, qB.h[:, r * TOK:(r + 1) * TOK], ago1[l], idxq, idxq.h[:, 3 * r:3 * r + 1])
                    gather_rows(k, kT, kT.h[:, r * TOK:(r + 1) * TOK], ago1[l], idxq, idxq.h[:, 3 * r + 1:3 * r + 2])
                k.op("dve", lambda e: e.memset(qT.h[64:128, :], 0.0), writes=[qT])
                k.op("dve", lambda e: e.memset(qB.h[0:64, :], 0.0), writes=[qB])
                qM = [qT, qB]
                with k.scope():
                    vst = [k.sbuf("vst%d" % i, [128, TOK], BF16) for i in range(2)]
                    ptb = k.psum("ptb", [128, 1024], BF16)
                    for r in range(8):
                        vs_ = vst[r % 2]
                        gather_rows(k, vs_, vs_.h[:, :], ago1[l], idxq, idxq.h[:, 3 * r + 2:3 * r + 3])
                        for i in range(8):
                            k.op("pe", lambda e, i=i, vs_=vs_: e.transpose(ptb.h[:, i * 128:(i + 1) * 128], vs_.h[:, i * 128:(i + 1) * 128], ident.h[:]),
                                 reads=[vs_, ident], writes=[ptb])
                        k.op("dve", lambda e, r=r: e.tensor_copy(out=vv.h[:, r * 8:(r + 1) * 8, :].rearrange("p a b -> p (a b)"), in_=ptb.h[:, :]),
                             reads=[ptb], writes=[vv])
                lt = k.sbuf("lt", [128, 128], F32)
                l2 = k.sbuf("l2", [128, 4], F32)
                k.op("dve", lambda e: e.tensor_tensor(out=lt.h[:, 0:64], in0=lamp.h[:, 0:64], in1=lamp.h[:, 64:128], op=ALU.mult), reads=[lamp], writes=[lt])
                k.op("dve", lambda e: e.tensor_tensor(out=lt.h[:, 64:128], in0=lamp.h[:, 128:192], in1=lamp.h[:, 192:256], op=ALU.mult), reads=[lamp, lt], writes=[lt])
                k.op("dve", lambda e: e.reduce_sum(out=l2.h[:, 0:2], in_=lt.h[:].rearrange("p (a b) -> p a b", a=2), axis=mybir.AxisListType.X),
                     reads=[lt], writes=[l2])
                k.op("act", lambda e: e.activation(out=l2.h[:, 0:2], in_=l2.h[:, 0:2], func=AF.Exp), reads=[l2], writes=[l2])
                k.op("dve", lambda e: e.tensor_tensor(out=l2.h[:, 2:3], in0=l2.h[:, 1:2], in1=l2.h[:, 0:1], op=ALU.subtract), reads=[l2], writes=[l2])
                k.op("dve", lambda e: e.tensor_scalar(out=l2.h[:, 3:4], in0=l2.h[:, 2:3], scalar1=-lam_init, scalar2=0.0, op0=ALU.add, op1=ALU.add), reads=[l2], writes=[l2])
                k.op("dve", lambda e: e.tensor_scalar(out=sgc.h[:], in0=sgc.h[:], scalar1=1.0 - lam_init, scalar2=1.0, op0=ALU.mult, op1=ALU.mult), reads=[sgc], writes=[sgc])
                pS = [k.psum("pS%d" % i, [128, 512], F32) for i in range(2)] + [c.ps[3]]
                pO = [k.psum("pO%d" % i, [128, 512], F32) for i in range(2)]
                pT = [k.sbuf("pT%d" % i, [128, 512], BF16) for i in range(6)]
                sb = [k.sbuf("sbias%d" % i, [128, 512], F32) for i in range(2)]
                Oacc = [k.sbuf("Oacc%d" % i, [128, 512], F32) for i in range(2)]
                Lacc = [k.sbuf("Lacc%d" % i, [128, 512], F32) for i in range(2)]
                Lsum = [k.sbuf("Lsum%d" % i, [128, 512], F32) for i in range(2)]
                dd = k.sbuf("dd", [128, 512], F32)
                pP = [k.sbuf("pP%d" % i, [128, 512], BF16) for i in range(2)]
                lst = {"pend": [None, None], "started": [False, False], "pi": 0}
                dob = [k.sbuf("dob%d" % i, [128, 512], BF16) for i in range(2)]
                cnt = {"s": 0, "p": 0, "b": 0}
                pL = c.ps[0:2]
                a2v = agi2[l].h.rearrange("(e tb) t -> e tb t", tb=8)
                def emit_qk(pr):
                    qi, m, kt, nkt, idx = pr
                    q0 = qi * 512
                    pb = m * 64
                    r = kt - 4 * qi
                    c0 = 128 * r if r > 0 else 0
                    n = 512 - c0
                    ps_ = pS[idx % 3]
                    k.mm(ps_, ps_.h[:, 0:n], [(kT.h[:, kt * 128:(kt + 1) * 128], qM[m].h[:, q0 + c0:q0 + 512])], reads=[kT, qM[m]])
                    pt = pT[idx % 6]
                    if r >= -1:
                        s_ = sb[cnt["b"] % 2]
                        cnt["b"] += 1
                        k.op("dve", lambda e: e.scalar_tensor_tensor(
                            out=s_.h[:, 0:n], in0=ps_.h[:, 0:n], scalar=0.125, in1=bm.h[:, r + 1, c0:512], op0=ALU.mult, op1=ALU.add),
                            reads=[ps_, bm], writes=[s_])
                        k.op("act", lambda e: e.activation(out=pt.h[:, 0:n], in_=s_.h[:, 0:n], func=AF.Exp), reads=[s_], writes=[pt])
                    else:
                        k.op("act", lambda e: e.activation(out=pt.h[:, 0:n], in_=ps_.h[:, 0:n], func=AF.Exp,
                                                           bias=cfar.h[:, 0:1], scale=0.125), reads=[ps_, cfar], writes=[pt])

                def emit_pv(pr):
                    qi, m, kt, nkt, idx = pr
                    r = kt - 4 * qi
                    c0 = 128 * r if r > 0 else 0
                    n = 512 - c0
                    pt = pT[idx % 6]
                    k.mm(pO[m], pO[m].h[:, c0:512], [(vv.h[:, kt, :], pt.h[:, 0:n])], reads=[vv, pt], start=(kt == 0), stop=(kt == nkt - 1))
                    if kt == 0:
                        lst["started"][m] = False
                    if r <= -2 and kt % 2 == 0 and r + 1 <= -2:
                        lst["pend"][m] = pt
                    elif r <= -2 and kt % 2 == 1 and lst["pend"][m] is not None:
                        p2 = pP[lst["pi"] % 2]
                        lst["pi"] += 1
                        pe_ = lst["pend"][m]
                        lst["pend"][m] = None
                        k.op("dve", lambda e: e.tensor_tensor(out=p2.h[:], in0=pe_.h[:], in1=pt.h[:], op=ALU.add), reads=[pe_, pt], writes=[p2])
                        k.mm(pL[m], pL[m].h[:, 0:512], [(c.onesb.h[:], p2.h[:])], reads=[c.onesb, p2], start=(not lst["started"][m]), stop=False)
                        lst["started"][m] = True
                    else:
                        k.mm(pL[m], pL[m].h[:, c0:512], [(c.onesb.h[:], pt.h[:, 0:n])], reads=[c.onesb, pt], start=(not lst["started"][m]), stop=(kt == nkt - 1))
                        lst["started"][m] = True
                    if kt == nkt - 1:
                        k.op("dve", lambda e: e.reciprocal(out=Lacc[m].h[:], in_=pL[m].h[:]), reads=[pL[m]], writes=[Lacc[m]])
                        k.op("dve", lambda e: e.tensor_tensor(out=Oacc[m].h[:], in0=pO[m].h[:], in1=Lacc[m].h[:], op=ALU.mult),
                             reads=[pO[m], Lacc[m]], writes=[Oacc[m]])
                        if m == 1:
                            emit_fin(qi)

                def emit_fin(qi):
                    q0 = qi * 512
                    k.op("dve", lambda e: e.scalar_tensor_tensor(out=dd.h[:], in0=Oacc[1].h[:], scalar=l2.h[:, 3:4], in1=Oacc[0].h[:],
                                                                 op0=ALU.mult, op1=ALU.add), reads=[Oacc[0], Oacc[1], l2], writes=[dd])
                    sq = c.sq[qi % 2]
                    k.op("act", lambda e: e.activation(out=sq.h[:], in_=dd.h[:], func=AF.Square), reads=[dd], writes=[sq])
                    p3 = c.ps[2]
                    k.mm(p3, p3.h[:], [(c.ones32.h[:], sq.h[:])], reads=[c.ones32, sq])
                    k.op("act", lambda e: e.activation(out=c.rb.h[:], in_=p3.h[:], func=AF.Sqrt, bias=c.eps5.h[:, 0:1], scale=1.0 / 128),
                         reads=[p3, c.eps5], writes=[c.rb])
                    k.op("dve", lambda e: e.reciprocal(out=c.rb.h[:], in_=c.rb.h[:]), reads=[c.rb], writes=[c.rb])
                    o = dob[qi % 2]
                    k.op("dve", lambda e: e.scalar_tensor_tensor(out=o.h[:], in0=dd.h[:], scalar=sgc.h[:, 0:1], in1=c.rb.h[:],
                                                                 op0=ALU.mult, op1=ALU.mult), reads=[dd, sgc, c.rb], writes=[o])
                    k.dma("sp", a2v[:, qi // 2, (qi % 2) * 512:(qi % 2) * 512 + 512], o.h[:], reads=[o], writes=[agi2[l]], semtile=o)

                seq = []
                for qi in range(16):
                    for m in range(2):
                        for kt in range(4 * (qi + 1)):
                            seq.append((qi, m, kt, 4 * (qi + 1), len(seq)))
                for i, pr in enumerate(seq):
                    emit_qk(pr)
                    if i >= 2:
                        emit_pv(seq[i - 2])
                emit_pv(seq[-2])
                emit_pv(seq[-1])
                collective(k, st, ccsems, "e2", agi2[l], ago2[l])
            with k.scope():
                slots_alloc(k, c)
                memT = k.sbuf("memTs", [128, 16, 256], F32)
                mnT = k.sbuf("mnT", [128, 16, 256], BF16)
                for ch in range(16):
                    load(k, "sp", memT, memT.h[:, ch, :], memT_d, memT_d.h[ch, :, :])
                rmsnorm_fm(k, c, memT, 16, g_mem, mnT, [(0, 256)], 1.0 / D, c.epsn)
                ckT = k.sbuf("ckT", [128, 4, 256], BF16)
                cvT = k.sbuf("cvT", [128, 4, 256], BF16)
                cvv = k.sbuf("cvv", [128, 2, 512], BF16)

                def ckv_h(oc, ri, t0, n, p):
                    dst = ckT if oc < 4 else cvT
                    k.op("act", lambda e: e.copy(out=dst.h[:, oc % 4, :], in_=p.h[:, 0:256]), reads=[p], writes=[dst])
                proj_fm(k, c, mnT, 16, T("w", wckv_d.h[l]), range(8), [(0, 256)], ckv_h)
                ptb = k.psum("ptbc", [128, 1024], BF16)
                for hh in range(4):
                    for mt in range(2):
                        k.op("pe", lambda e, hh=hh, mt=mt: e.transpose(ptb.h[:, 0:128], cvT.h[:, hh, mt * 128:(mt + 1) * 128], ident.h[:]),
                             reads=[cvT, ident], writes=[ptb])
                        k.op("dve", lambda e, hh=hh, mt=mt: e.tensor_copy(out=cvv.h[:, mt, hh * 128:(hh + 1) * 128], in_=ptb.h[:, 0:128]),
                             reads=[ptb], writes=[cvv])
                for r in range(8):
                    gather_rows(k, hT, hT.h[:, 8 + r, 2:NT], ago2[l], idxd, idxd.h[:, r:r + 1])
                proj_fm(k, c, hTo, 16, T("w", wout_d.h[l]), range(16), R2, add_x)
                rmsnorm_fm(k, c, xT, 16, g_cross, hTo, R2, 1.0 / D, c.epsn)
                cqT = k.sbuf("cqT", [128, 4, TOK], BF16)
                coT = k.sbuf("coT", [128, 4, TOK], BF16)

                def cq_h(oc, ri, t0, n, p):
                    k.op("act", lambda e: e.copy(out=cqT.h[:, oc, t0:t0 + n], in_=p.h[:, 0:n]), reads=[p], writes=[cqT])
                proj_fm(k, c, hTo, 16, T("w", wcq_d.h[l]), range(4), R2, cq_h)
                pTc = [k.sbuf("pTc%d" % i, [128, 512], BF16) for i in range(2)]
                rl = k.sbuf("rl", [128, 512], F32)
                ci = 0
                for hh in range(4):
                    for (t0, n) in R2:
                        pO_ = next_ps(c)
                        pL_ = next_ps(c)
                        for mt in range(2):
                            pS_ = next_ps(c)
                            k.mm(pS_, pS_.h[:, 0:n], [(ckT.h[:, hh, mt * 128:(mt + 1) * 128], cqT.h[:, hh, t0:t0 + n])], reads=[ckT, cqT])
                            pt = pTc[ci % 2]
                            ci += 1
                            k.op("act", lambda e, pS_=pS_, pt=pt: e.activation(out=pt.h[:, 0:n], in_=pS_.h[:, 0:n], func=AF.Exp, scale=128.0 ** -0.5),
                                 reads=[pS_], writes=[pt])
                            k.mm(pO_, pO_.h[:, 0:n], [(cvv.h[:, mt, hh * 128:(hh + 1) * 128], pt.h[:, 0:n])], reads=[cvv, pt], start=(mt == 0), stop=(mt == 1))
                            k.mm(pL_, pL_.h[:, 0:n], [(c.onesb.h[:], pt.h[:, 0:n])], reads=[c.onesb, pt], start=(mt == 0), stop=(mt == 1))
                        k.op("dve", lambda e, pL_=pL_: e.reciprocal(out=rl.h[:, 0:n], in_=pL_.h[:, 0:n]), reads=[pL_], writes=[rl])
                        k.op("dve", lambda e, pO_=pO_, hh=hh: e.tensor_tensor(out=coT.h[:, hh, t0:t0 + n], in0=pO_.h[:, 0:n], in1=rl.h[:, 0:n], op=ALU.mult),
                             reads=[pO_, rl], writes=[coT])
                proj_fm(k, c, coT, 4, T("w", wco_d.h[l]), range(16), R2, add_x)
                rmsnorm_fm(k, c, xT, 16, g_ffn, hTo, R2, 1.0 / D, c.epsn)
                k.dma("sp", agi3[l].h.rearrange("p (kc j) -> p kc j", j=2), hT.h[:, :, NT - 2:NT], reads=[hT], writes=[agi3[l]], semtile=hT)
                collective(k, st, ccsems, "e3", agi3[l], ago3[l])
                halo3 = k.sbuf("halo3", [128, 32], BF16)
                gather_rows(k, halo3, halo3.h[:, :], ago3[l], idxh, idxh.h[:, 0:1])
                k.op("dve", lambda e: e.tensor_scalar(out=hT.h[:, :, 0:2], in0=halo3.h[:].rearrange("p (kc j) -> p kc j", j=2), scalar1=flag.h[:, 0:1],
                                                      scalar2=1.0, op0=ALU.mult, op1=ALU.mult), reads=[halo3, flag, hT], writes=[hT])
            with k.scope():
                slots_alloc(k, c)
                cf = k.sbuf("cfs", [128, 88, 4], F32)
                load(k, "sp", cf, cf.h[:], cf_d, cf_d.h[l, :, :, :])
                gT = k.sbuf("gT", [128, 44, 512], BF16)
                pU = [k.psum("pU%d" % i, [128, 1024], F32) for i in range(2)]
                t0b = [k.sbuf("t0b%d" % i, [128, 512], F32) for i in range(2)]
                wup_l = T("w", wup_d.h[l])
                s1 = [k.sbuf("s1_%d" % i, [128, 512], F32) for i in range(4)]
                uh = k.sbuf("uh", [128, 88, 2], F32)
                puh = []
                for i in range(2):
                    t_ = T("puh%d" % i, pU[i].h)
                    k.tiles.append(t_)
                    puh.append(t_)
                for half in range(2):
                    hb = 512 * half
                    jobs = []
                    for g in range(11):
                        jobs.append((0, g))
                        jobs.append((1, g))
                    nxt = load_slot(k, c, wup_l, 0, 16, (jobs[0][0] * 44 + 4 * jobs[0][1]) * 128)
                    ui = 0
                    for ji, (which, g) in enumerate(jobs):
                        s = nxt
                        if ji + 1 < len(jobs):
                            nxt = load_slot(k, c, wup_l, 0, 16, (jobs[ji + 1][0] * 44 + 4 * jobs[ji + 1][1]) * 128)
                        for j in range(4):
                            oc = which * 44 + 4 * g + j
                            pu = pU[ui % 2]
                            tb = t0b[ui % 2]
                            ui += 1
                            ph = puh[(ui - 1) % 2]
                            if half == 0:
                                k.mm(ph, pu.h[:, 510:512], [(s.h[:, kc * 512 + j * 128:kc * 512 + (j + 1) * 128], hT.h[:, kc, hb:hb + 2]) for kc in range(16)], reads=[s, hT])
                            else:
                                k.op("act", lambda e: e.copy(out=pu.h[:, 510:512], in_=uh.h[:, oc, :]), reads=[uh], writes=[ph])
                            k.mm(pu, pu.h[:, 512:1024], [(s.h[:, kc * 512 + j * 128:kc * 512 + (j + 1) * 128], hT.h[:, kc, hb + 2:hb + 514]) for kc in range(16)], reads=[s, hT])
                            if half == 0:
                                k.op("act", lambda e: e.copy(out=uh.h[:, oc, :], in_=pu.h[:, 1022:1024]), reads=[pu], writes=[uh])
                            k.op("act", lambda e: e.activation(out=tb.h[:], in_=pu.h[:, 512:1024], func=AF.Identity,
                                                               bias=cf.h[:, oc, 3:4], scale=cf.h[:, oc, 2:3]), reads=[pu, cf], writes=[tb])
                            k.op("dve", lambda e: e.scalar_tensor_tensor(out=tb.h[:], in0=pu.h[:, 511:1023], scalar=cf.h[:, oc, 1:2], in1=tb.h[:],
                                                                         op0=ALU.mult, op1=ALU.add), reads=[pu, ph, cf, tb], writes=[tb])
                            k.op("dve", lambda e: e.scalar_tensor_tensor(out=tb.h[:], in0=pu.h[:, 510:1022], scalar=cf.h[:, oc, 0:1], in1=tb.h[:],
                                                                         op0=ALU.mult, op1=ALU.add), reads=[pu, ph, cf, tb], writes=[tb])
                            if which == 0:
                                k.op("act", lambda e: e.activation(out=s1[j].h[:], in_=tb.h[:], func=AF.Silu), reads=[tb], writes=[s1[j]])
                            else:
                                k.op("dve", lambda e: e.tensor_tensor(out=gT.h[:, 4 * g + j, :], in0=s1[j].h[:], in1=tb.h[:], op=ALU.mult),
                                     reads=[s1[j], tb], writes=[gT])

                    def add_x2(oc, ri, t0, n, p, half=half):
                        a0 = 512 * half
                        k.op("dve", lambda e: e.tensor_tensor(out=xT.h[:, oc, a0:a0 + 512], in0=xT.h[:, oc, a0:a0 + 512], in1=p.h[:, 0:512], op=ALU.add),
                             reads=[xT, p], writes=[xT])
                    proj_fm(k, c, gT, 44, T("w", wdn_d.h[l]), range(16), [(0, 512)], add_x2)
        with k.scope():
            yT = k.sbuf("yT", [128, 16, 512], F32)
            for (t0, n) in R2:
                yv = ShiftView(yT, -t0)
                rmsnorm_fm(k, c, xT, 16, g_fin, yv, [(t0, n)], 1.0 / D, c.epsn)
                for ch in range(16):
                    k.dma("sp", yo_d.h[ch, :, t0:t0 + n], yT.h[:, ch, :], reads=[yT], writes=[yo_d], semtile=yT)
        k.finish("sp")
    return nc


def _t5_bucket(n):
    n = np.asarray(n, np.int64)
    nf = np.maximum(n, 1).astype(np.float32)
    large = 16 + (np.log(nf / np.float32(16)) / np.float32(math.log(128 / 16)) * np.float32(16)).astype(np.int32)
    large = np.minimum(large, 31)
    return np.where(n < 16, n, large)


def _cols(v, n):
    return np.ascontiguousarray(np.asarray(v, np.float32).reshape(n, 128).T)


def _fm(a):
    t, f = a.shape
    return np.ascontiguousarray(a.T.reshape(f // 128, 128, t))


_PROGS = {}


def kernel(x, mem, rel_bias_table, g_mix, w_in, conv_a_w, conv_a_b, ln_a_g, ln_a_b,
           diff_lambda, subln_g, w_out, g_cross, g_mem, w_cq, w_ckv, w_co,
           g_ffn, w_up, conv_f_w, conv_f_b, w_down, g_final):
    f32 = np.float32
    x = np.asarray(x, f32)[0]
    memT = _fm(np.asarray(mem, f32)[0])
    tbl = np.asarray(rel_bias_table, f32)
    depth = int(np.asarray(w_in).shape[0])
    cores = list(range(NCORES))
    kl = np.arange(128)[:, None]
    ql = np.arange(512)[None, :]
    bms, masks = [], []
    for r in range(-1, 4):
        rel = ql - (128 * r + kl)
        bms.append(_t5_bucket(np.clip(rel, 0, S - 1)))
        masks.append(np.where(rel >= 0, 0.0, NEG).astype(f32))
    masks = np.stack(masks)
    A = lambda a: np.ascontiguousarray(np.asarray(a, f32))
    gcols = np.stack([np.stack([_cols(g_mix[l], 16), _cols(g_cross[l], 16), _cols(g_mem[l], 16), _cols(g_ffn[l], 16), _cols(g_final, 16)], axis=1)
                      for l in range(depth)])
    convw = np.stack([np.asarray(conv_a_w[l], f32).reshape(31, 8, 128).transpose(2, 1, 0) for l in range(depth)])
    cvec = np.stack([np.stack([_cols(conv_a_b[l], 8), _cols(ln_a_g[l], 8), _cols(ln_a_b[l], 8)], axis=1) for l in range(depth)])
    lamp = np.stack([np.broadcast_to(np.asarray(diff_lambda[l], f32).reshape(1, 256), (128, 256)) for l in range(depth)])
    sublng = np.stack([np.asarray(subln_g[l], f32).reshape(128, 1) for l in range(depth)])
    convf = np.stack([np.stack([_cols(np.asarray(conv_f_w[l], f32)[0], 88), _cols(np.asarray(conv_f_w[l], f32)[1], 88),
                                _cols(np.asarray(conv_f_w[l], f32)[2], 88), _cols(conv_f_b[l], 88)], axis=2) for l in range(depth)])
    shared = {"memT": memT, "w_in": A(w_in), "w_out": A(w_out), "w_cq": A(w_cq), "w_ckv": A(w_ckv), "w_co": A(w_co),
              "w_up": A(w_up), "w_down": A(w_down), "gcols": A(gcols), "convw": A(convw), "cvec": A(cvec), "lamp": A(lamp),
              "sublng": A(sublng), "convf": A(convf), "mask": masks}
    p = np.arange(128, dtype=np.int64)
    in_maps = []
    for c in cores:
        h = c
        idxq = np.zeros((128, 24), np.uint32)
        for r in range(8):
            for j in range(3):
                idxq[:, 3 * r + j] = r * 3072 + j * 1024 + h * 128 + p
        idxd = np.zeros((128, 8), np.uint32)
        for r in range(8):
            idxd[:, r] = r * 1024 + p * 8 + c
        prev = c - 1 if c > 0 else c
        m = dict(shared)
        m.update({
            "xT": _fm(x[c * TOK:(c + 1) * TOK]),
            "bm": np.ascontiguousarray(np.stack([tbl[b, h] for b in bms]).astype(f32)),
            "cfar": np.full((128, 1), tbl[31, h], f32),
            "flag": np.full((128, 1), 0.0 if c == 0 else 1.0, f32),
            "idxq": idxq, "idxd": idxd,
            "idxh": (prev * 128 + p).astype(np.uint32).reshape(128, 1),
        })
        in_maps.append(m)
    if depth not in _PROGS:
        _PROGS[depth] = build_fused(depth)
    res = run_bass_kernel_spmd(_PROGS[depth], in_maps, core_ids=cores).results
    yT = np.concatenate([res[c]["yoT"] for c in cores], axis=2)
    return np.ascontiguousarray(yT.reshape(D, S).T).reshape(1, S, D).astype(f32)
```
